# Optimizing a Trainium2 kernel written in Bass

```python
import math
import jax, jax.numpy as jnp
from jax import lax
import numpy as np

D_MODEL = 1024
BATCH = 8
SEQ = 4096
DEPTH = 2

CTX_LEN = 256
GRID_W = 64
ROPE_BASE = 10000.0
Q_BLOCK = 128
NORM_EPS = 1e-6
N_BRANCH = 3

MLA_HEADS = 8
MLA_Q_RANK = 256
MLA_KV_RANK = 128
MLA_NOPE = 64
MLA_ROPE = 32
MLA_V = 64
DIFF_HEADS = 4
DIFF_QK = 64
DIFF_V = 2 * DIFF_QK
S5_CH = 512
S5_GROUP = 16
S5_GROUPS = S5_CH // S5_GROUP
S5_STATE = 64
S5_DT_MIN = 1e-3
S5_DT_MAX = 1e-1
FFN_HIDDEN = 2816
CONV_W = 3

IN_SIZES = (MLA_Q_RANK, MLA_KV_RANK, MLA_ROPE,
            DIFF_HEADS * 2 * DIFF_QK, DIFF_HEADS * 2 * DIFF_QK, DIFF_HEADS * DIFF_V,
            S5_CH, N_BRANCH * D_MODEL)
IN_TOTAL = sum(IN_SIZES)

kernel_name = "hybrid_mla_diffattn_s5_convglu_prefix_dit"

F32 = jnp.float32


def _rms(x):
    xf = x.astype(F32)
    return xf * lax.rsqrt(jnp.mean(xf * xf, axis=-1, keepdims=True) + NORM_EPS)


def _modulate(x, shift, scale):
    return (_rms(x) * (1.0 + scale) + shift).astype(x.dtype)


def _grid_positions(n_tokens):
    rows = n_tokens // GRID_W
    r, c = jnp.meshgrid(jnp.arange(rows), jnp.arange(GRID_W), indexing="ij")
    return r.reshape(-1), c.reshape(-1)


def _rope_1d(x, pos):
    half = x.shape[-1] // 2
    inv = ROPE_BASE ** (-jnp.arange(half, dtype=F32) / half)
    ang = pos.astype(F32)[:, None] * inv[None, :]
    cos, sin = jnp.cos(ang), jnp.sin(ang)
    xf = x.astype(F32)
    x1, x2 = xf[..., :half], xf[..., half:]
    return jnp.concatenate([x1 * cos - x2 * sin, x2 * cos + x1 * sin], axis=-1).astype(x.dtype)


def _axial_rope(x, row, col):
    d = x.shape[-1] // 2
    return jnp.concatenate([_rope_1d(x[..., :d], row), _rope_1d(x[..., d:], col)], axis=-1)


def _split_in(p):
    idx = np.cumsum(IN_SIZES)[:-1].tolist()
    return jnp.split(p, idx, axis=-1)


def _merge_heads(y):
    b, h, l, d = y.shape
    return y.transpose(0, 2, 1, 3).reshape(b, l, h * d)


def _softmax_probs(q, k, scale):
    s = jnp.einsum("bhqd,bhkd->bhqk", q, k).astype(F32) * scale
    return jax.nn.softmax(s, axis=-1)


def _sweep_query_blocks(block_fn, qs):
    b, h, lq, _ = qs[0].shape
    nb = lq // Q_BLOCK
    qb = tuple(q.reshape(b, h, nb, Q_BLOCK, q.shape[-1]).transpose(2, 0, 1, 3, 4) for q in qs)
    out = lax.map(block_fn, qb)
    return out.transpose(1, 2, 0, 3, 4).reshape(b, h, lq, out.shape[-1])


def _attend(q, k, v, scale):
    def block(qs):
        p = _softmax_probs(qs[0], k, scale)
        return jnp.einsum("bhqk,bhkd->bhqd", p.astype(v.dtype), v)
    return _sweep_query_blocks(block, (q,))


def _diff_attend(q1, q2, k1, k2, v, lam, scale):
    def block(qs):
        p = _softmax_probs(qs[0], k1, scale) - lam * _softmax_probs(qs[1], k2, scale)
        return jnp.einsum("bhqk,bhkd->bhqd", p.astype(v.dtype), v)
    return _sweep_query_blocks(block, (q1, q2))


def _mla_q(c_q, q_norm, w_uq, pos):
    b, l, _ = c_q.shape
    q = (_rms(c_q) * q_norm).astype(c_q.dtype) @ w_uq
    q = q.reshape(b, l, MLA_HEADS, MLA_NOPE + MLA_ROPE).transpose(0, 2, 1, 3)
    if pos is not None:
        q = jnp.concatenate([q[..., :MLA_NOPE], _axial_rope(q[..., MLA_NOPE:], *pos)], axis=-1)
    return q


def _mla_kv(c_kv, k_rope, kv_norm, w_ukv, pos):
    b, l, _ = c_kv.shape
    kv = (_rms(c_kv) * kv_norm).astype(c_kv.dtype) @ w_ukv
    kv = kv.reshape(b, l, MLA_HEADS, MLA_NOPE + MLA_V).transpose(0, 2, 1, 3)
    if pos is not None:
        k_rope = _axial_rope(k_rope, *pos)
    k_rope = jnp.broadcast_to(k_rope[:, None], (b, MLA_HEADS, l, MLA_ROPE)).astype(kv.dtype)
    k = jnp.concatenate([kv[..., :MLA_NOPE], k_rope], axis=-1)
    return k, kv[..., MLA_NOPE:]


def _mla_branch(pc, pl, pos, need_ctx, q_norm, w_uq, kv_norm, w_ukv, w_o):
    scale = (MLA_NOPE + MLA_ROPE) ** -0.5
    kc, vc = _mla_kv(pc[1], pc[2], kv_norm, w_ukv, None)
    kl, vl = _mla_kv(pl[1], pl[2], kv_norm, w_ukv, pos)
    ql = _mla_q(pl[0], q_norm, w_uq, pos)
    k_all = jnp.concatenate([kc, kl], axis=2)
    v_all = jnp.concatenate([vc, vl], axis=2)
    y_l = _merge_heads(_attend(ql, k_all, v_all, scale)) @ w_o
    y_c = None
    if need_ctx:
        qc = _mla_q(pc[0], q_norm, w_uq, None)
        y_c = _merge_heads(_attend(qc, kc, vc, scale)) @ w_o
    return y_c, y_l


def _diff_qk(t):
    b, l, _ = t.shape
    return t.reshape(b, l, DIFF_HEADS, 2, DIFF_QK).transpose(3, 0, 2, 1, 4)


def _diff_v(t):
    b, l, _ = t.shape
    return t.reshape(b, l, DIFF_HEADS, DIFF_V).transpose(0, 2, 1, 3)


def _diff_branch(pc, pl, pos, lam_init, need_ctx, diff_lambda, diff_subln, w_o):
    lf = diff_lambda.astype(F32)
    lam = jnp.exp(jnp.sum(lf[0] * lf[1])) - jnp.exp(jnp.sum(lf[2] * lf[3])) + lam_init
    scale = DIFF_QK ** -0.5
    kc, vc = _diff_qk(pc[1]), _diff_v(pc[2])
    ql = _axial_rope(_diff_qk(pl[0]), *pos)
    kl = _axial_rope(_diff_qk(pl[1]), *pos)
    vl = _diff_v(pl[2])
    k_all = jnp.concatenate([kc, kl], axis=3)
    v_all = jnp.concatenate([vc, vl], axis=2)

    def finish(o):
        o = (_rms(o) * diff_subln * (1.0 - lam_init)).astype(vl.dtype)
        return _merge_heads(o) @ w_o

    y_l = finish(_diff_attend(ql[0], ql[1], k_all[0], k_all[1], v_all, lam, scale))
    y_c = None
    if need_ctx:
        qc = _diff_qk(pc[0])
        y_c = finish(_diff_attend(qc[0], qc[1], kc[0], kc[1], vc, lam, scale))
    return y_c, y_l


def _s5_discretize(lam_re, lam_im, log_dt, b_re, b_im):
    lam = lax.complex(jnp.minimum(lam_re.astype(F32), -1e-4), lam_im.astype(F32))
    dt = jnp.exp(log_dt.astype(F32))[:, None]
    lam_bar = jnp.exp(lam * dt)
    b = lax.complex(b_re.astype(F32), b_im.astype(F32))
    b_bar = ((lam_bar - 1.0) / lam)[..., None] * b
    return lam_bar, b_bar


def _s5_combine(e1, e2):
    a1, b1 = e1
    a2, b2 = e2
    return a1 * a2, a2 * b1 + b2


def _s5_scan(u, lam_bar, b_bar, h0, reverse):
    l = u.shape[1]
    bu = lax.complex(jnp.einsum("blgh,gnh->blgn", u, jnp.real(b_bar)),
                     jnp.einsum("blgh,gnh->blgn", u, jnp.imag(b_bar)))
    if h0 is not None:
        bu = bu.at[:, -1 if reverse else 0].add(lam_bar * h0)
    a = jnp.broadcast_to(lam_bar[None, None], (1, l) + lam_bar.shape)
    _, h = lax.associative_scan(_s5_combine, (a, bu), reverse=reverse, axis=1)
    return h


def _s5_readout(h, c_re, c_im):
    return (jnp.einsum("blgn,ghn->blgh", jnp.real(h), c_re.astype(F32))
            - jnp.einsum("blgn,ghn->blgh", jnp.imag(h), c_im.astype(F32)))


def _s5_branch(u_ctx, u_lat, need_ctx, lam_re, lam_im, log_dt, b_re, b_im, c_re, c_im, d_skip, w_glu):
    bsz = u_lat.shape[0]
    uc = u_ctx.astype(F32).reshape(bsz, u_ctx.shape[1], S5_GROUPS, S5_GROUP)
    ul = u_lat.astype(F32).reshape(bsz, u_lat.shape[1], S5_GROUPS, S5_GROUP)
    ys_l, ys_c = [], []
    for direction, reverse in enumerate((False, True)):
        lam_bar, b_bar = _s5_discretize(lam_re[direction], lam_im[direction], log_dt[direction],
                                        b_re[direction], b_im[direction])
        hc = _s5_scan(uc, lam_bar, b_bar, None, reverse)
        h0 = hc[:, 0] if reverse else hc[:, -1]
        hl = _s5_scan(ul, lam_bar, b_bar, h0, reverse)
        ys_l.append(_s5_readout(hl, c_re[direction], c_im[direction]))
        if need_ctx:
            ys_c.append(_s5_readout(hc, c_re[direction], c_im[direction]))

    def glu(y, u):
        l = u.shape[1]
        z = jax.nn.gelu(y.reshape(bsz, l, S5_CH) + d_skip.astype(F32) * u.reshape(bsz, l, S5_CH))
        zz = z.astype(w_glu.dtype) @ w_glu
        return zz[..., :D_MODEL] * jax.nn.sigmoid(zz[..., D_MODEL:])

    y_l = glu(ys_l[0] + ys_l[1], ul)
    y_c = glu(ys_c[0] + ys_c[1], uc) if need_ctx else None
    return y_c, y_l


def _mixing_block(a_ctx, a_lat, pos, lam_init, need_ctx, w_in,
                  mla_q_norm, mla_w_uq, mla_kv_norm, mla_w_ukv, mla_w_o,
                  diff_lambda, diff_subln, diff_w_o,
                  s5_lambda_re, s5_lambda_im, s5_log_dt, s5_b_re, s5_b_im, s5_c_re, s5_c_im,
                  s5_d, s5_w_glu, w_out):
    pc = _split_in(a_ctx @ w_in)
    pl = _split_in(a_lat @ w_in)
    mla_c, mla_l = _mla_branch(pc[0:3], pl[0:3], pos, need_ctx,
                               mla_q_norm, mla_w_uq, mla_kv_norm, mla_w_ukv, mla_w_o)
    diff_c, diff_l = _diff_branch(pc[3:6], pl[3:6], pos, lam_init, need_ctx,
                                  diff_lambda, diff_subln, diff_w_o)
    s5_c, s5_l = _s5_branch(pc[6], pl[6], need_ctx, s5_lambda_re, s5_lambda_im, s5_log_dt,
                            s5_b_re, s5_b_im, s5_c_re, s5_c_im, s5_d, s5_w_glu)

    def merge(gate_logits, y_a, y_b, y_s):
        b, l, _ = gate_logits.shape
        g = jax.nn.sigmoid(gate_logits.astype(F32)).reshape(b, l, N_BRANCH, D_MODEL)
        m = g[:, :, 0] * y_a + g[:, :, 1] * y_b + g[:, :, 2] * y_s
        return m.astype(w_out.dtype) @ w_out

    y_lat = merge(pl[7], mla_l, diff_l, s5_l)
    y_ctx = merge(pc[7], mla_c, diff_c, s5_c) if need_ctx else None
    return y_ctx, y_lat


def _conv_ffn(h, w_up, conv_w, conv_b, w_down):
    a, g = jnp.split(h @ w_up, 2, axis=-1)
    a = lax.conv_general_dilated(a, conv_w.astype(a.dtype)[:, None, :], window_strides=(1,),
                                 padding=((CONV_W // 2, CONV_W // 2),),
                                 dimension_numbers=("NWC", "WIO", "NWC"),
                                 feature_group_count=FFN_HIDDEN) + conv_b
    return (jax.nn.gelu(a) * g) @ w_down


def setup_inputs(seed: int = 0) -> dict:
    key = jax.random.key(seed)
    ks = iter(jax.random.split(key, 40))
    L, D = DEPTH, D_MODEL

    def nrm(shape, scale):
        return jax.random.normal(next(ks), shape, F32) * scale

    n_idx = jnp.arange(S5_STATE, dtype=F32)
    return {
        "x": nrm((BATCH, SEQ, D), 1.0),
        "c": nrm((BATCH, D), 1.0),
        "ctx": nrm((BATCH, CTX_LEN, D), 1.0),
        "c_ctx": nrm((D,), 1.0),
        "w_mod": nrm((L, D, 6 * D), 0.5 * D ** -0.5),
        "b_mod": nrm((L, 6 * D), 0.02),
        "w_in": nrm((L, D, IN_TOTAL), D ** -0.5),
        "mla_q_norm": 1.0 + nrm((L, MLA_Q_RANK), 0.02),
        "mla_w_uq": nrm((L, MLA_Q_RANK, MLA_HEADS * (MLA_NOPE + MLA_ROPE)), MLA_Q_RANK ** -0.5),
        "mla_kv_norm": 1.0 + nrm((L, MLA_KV_RANK), 0.02),
        "mla_w_ukv": nrm((L, MLA_KV_RANK, MLA_HEADS * (MLA_NOPE + MLA_V)), MLA_KV_RANK ** -0.5),
        "mla_w_o": nrm((L, MLA_HEADS * MLA_V, D), (MLA_HEADS * MLA_V) ** -0.5),
        "diff_lambda": nrm((L, 4, DIFF_QK), 0.1),
        "diff_subln": 1.0 + nrm((L, DIFF_V), 0.02),
        "diff_w_o": nrm((L, DIFF_HEADS * DIFF_V, D), (DIFF_HEADS * DIFF_V) ** -0.5),
        "s5_lambda_re": -0.5 * jnp.exp(nrm((L, 2, S5_GROUPS, S5_STATE), 0.05)),
        "s5_lambda_im": math.pi * n_idx + nrm((L, 2, S5_GROUPS, S5_STATE), 0.01),
        "s5_log_dt": jax.random.uniform(next(ks), (L, 2, S5_GROUPS), F32,
                                        math.log(S5_DT_MIN), math.log(S5_DT_MAX)),
        "s5_b_re": nrm((L, 2, S5_GROUPS, S5_STATE, S5_GROUP), (2 * S5_GROUP) ** -0.5),
        "s5_b_im": nrm((L, 2, S5_GROUPS, S5_STATE, S5_GROUP), (2 * S5_GROUP) ** -0.5),
        "s5_c_re": nrm((L, 2, S5_GROUPS, S5_GROUP, S5_STATE), S5_STATE ** -0.5),
        "s5_c_im": nrm((L, 2, S5_GROUPS, S5_GROUP, S5_STATE), S5_STATE ** -0.5),
        "s5_d": nrm((L, S5_CH), 1.0),
        "s5_w_glu": nrm((L, S5_CH, 2 * D), S5_CH ** -0.5),
        "w_out": nrm((L, D, D), D ** -0.5),
        "ffn_w_up": nrm((L, D, 2 * FFN_HIDDEN), D ** -0.5),
        "ffn_conv_w": nrm((L, CONV_W, FFN_HIDDEN), CONV_W ** -0.5),
        "ffn_conv_b": nrm((L, FFN_HIDDEN), 0.02),
        "ffn_w_down": nrm((L, FFN_HIDDEN, D), FFN_HIDDEN ** -0.5),
        "final_norm": 1.0 + nrm((D,), 0.02),
    }


def reference(x, c, ctx, c_ctx, w_mod, b_mod, w_in,
              mla_q_norm, mla_w_uq, mla_kv_norm, mla_w_ukv, mla_w_o,
              diff_lambda, diff_subln, diff_w_o,
              s5_lambda_re, s5_lambda_im, s5_log_dt, s5_b_re, s5_b_im, s5_c_re, s5_c_im,
              s5_d, s5_w_glu, w_out,
              ffn_w_up, ffn_conv_w, ffn_conv_b, ffn_w_down, final_norm):
    pos = _grid_positions(x.shape[1])
    h_ctx = ctx
    silu_c = jax.nn.silu(c.astype(F32))
    silu_cc = jax.nn.silu(c_ctx.astype(F32))
    for li in range(DEPTH):
        need_ctx = li < DEPTH - 1
        lam_init = 0.8 - 0.6 * math.exp(-0.3 * li)
        mod_l = (silu_c @ w_mod[li] + b_mod[li])[:, None, :]
        mod_c = (silu_cc @ w_mod[li] + b_mod[li])[None, None, :]
        sh1, sc1, g1, sh2, sc2, g2 = jnp.split(mod_l, 6, axis=-1)
        csh1, csc1, cg1, csh2, csc2, cg2 = jnp.split(mod_c, 6, axis=-1)

        y_ctx, y_lat = _mixing_block(
            _modulate(h_ctx, csh1, csc1), _modulate(x, sh1, sc1), pos, lam_init, need_ctx,
            w_in[li], mla_q_norm[li], mla_w_uq[li], mla_kv_norm[li], mla_w_ukv[li], mla_w_o[li],
            diff_lambda[li], diff_subln[li], diff_w_o[li],
            s5_lambda_re[li], s5_lambda_im[li], s5_log_dt[li], s5_b_re[li], s5_b_im[li],
            s5_c_re[li], s5_c_im[li], s5_d[li], s5_w_glu[li], w_out[li])
        x = x + (g1 * y_lat).astype(x.dtype)
        x = x + (g2 * _conv_ffn(_modulate(x, sh2, sc2), ffn_w_up[li], ffn_conv_w[li],
                                ffn_conv_b[li], ffn_w_down[li])).astype(x.dtype)
        if need_ctx:
            h_ctx = h_ctx + (cg1 * y_ctx).astype(h_ctx.dtype)
            h_ctx = h_ctx + (cg2 * _conv_ffn(_modulate(h_ctx, csh2, csc2), ffn_w_up[li], ffn_conv_w[li],
                                             ffn_conv_b[li], ffn_w_down[li])).astype(h_ctx.dtype)
    return (_rms(x) * final_norm).astype(x.dtype)
```

```python
import math, contextlib
import numpy as np
import concourse.bass as bass
import concourse.mybir as mybir
from concourse.bass_utils import run_bass_kernel_spmd

F32 = mybir.dt.float32
BF16 = mybir.dt.bfloat16
I32 = mybir.dt.int32
AF = mybir.ActivationFunctionType
ALU = mybir.AluOpType

D = 1024
SEQ = 4096
NCTX = 256
TT = SEQ + NCTX
L = 2
NH_M = 8
NH_D = 4
FH = 2816
NFT = FH // 128
TC = 16
NCH = TT // TC
EPS = 1e-6
PI = math.pi


class T:
    __slots__ = ("w", "r")

    def __init__(self):
        self.w = None
        self.r = {}


class Sched:
    ENG = ("pe", "act", "dve", "pool", "sp")

    def __init__(self, nc, stack, n_dma_sems=24):
        self.nc = nc
        self.q = {e: [] for e in self.ENG}
        self.cnt = {e: 0 for e in self.ENG}
        self.known = {e: {} for e in self.ENG}
        self.n_dma = n_dma_sems
        self.dma_val = [0] * n_dma_sems
        self.dma_rr = 0
        self.NEPOCH = 18
        spsem = stack.enter_context(nc.semaphore("s_sp"))
        self.esems = []
        for i in range(self.NEPOCH):
            d_ = {e: stack.enter_context(nc.semaphore("s%d_%s" % (i, e))) for e in self.ENG if e != "sp"}
            d_["sp"] = spsem
            self.esems.append(d_)
        self.epoch = 0
        self.dsem = [stack.enter_context(nc.semaphore("d%d" % i)) for i in range(n_dma_sems)]
        self.ninst = 0

    def _deps(self, eng, R, W):
        deps = {}
        ep = self.epoch
        for t in R:
            if t.w is not None and t.w[2] == ep:
                k, v, _ = t.w
                if deps.get(k, 0) < v:
                    deps[k] = v
        for t in W:
            if t.w is not None and t.w[2] == ep:
                k, v, _ = t.w
                if deps.get(k, 0) < v:
                    deps[k] = v
            for k, (v, e_) in t.r.items():
                if e_ == ep and deps.get(k, 0) < v:
                    deps[k] = v
        out = []
        kn = self.known[eng]
        for k, v in deps.items():
            if k == "pe" and eng == "pe":
                continue
            if kn.get(k, 0) < v:
                kn[k] = v
                out.append((k, v))
        return out

    def op(self, eng, fn, R=(), W=(), inc=True):
        waits = self._deps(eng, R, W)
        if inc:
            self.cnt[eng] += 1
            c = self.cnt[eng]
        else:
            c = self.cnt[eng] + 1
        self.q[eng].append((waits, fn, inc))
        self.ninst += 1
        ep = self.epoch
        for t in R:
            o_ = t.r.get(eng)
            if o_ is None or o_[1] != ep or o_[0] < c:
                t.r[eng] = (c, ep)
        for t in W:
            t.w = (eng, c, ep)
            t.r = {}

    def dma(self, eng, out, in_, R=(), W=(), **kw):
        s = self.dma_rr
        self.dma_rr = (self.dma_rr + 1) % self.n_dma
        key = ("dma", s)
        waits = self._deps(eng, R, W)
        kn = self.known[eng]
        if kn.get(key, 0) < self.dma_val[s]:
            kn[key] = self.dma_val[s]
            waits.append((key, self.dma_val[s]))
        self.dma_val[s] += 16
        v = self.dma_val[s]
        self.q[eng].append((waits, ("dma", out, in_, s, kw), False))
        self.ninst += 1
        for t in R:
            t.r[key] = (v, self.epoch)
        for t in W:
            t.w = (key, v, self.epoch)
            t.r = {}

    def barrier(self):
        for e in self.ENG:
            kn = self.known[e]
            waits = []
            for f in self.ENG:
                if f != e and kn.get(f, 0) < self.cnt[f]:
                    kn[f] = self.cnt[f]
                    waits.append((f, self.cnt[f]))
            for s in range(self.n_dma):
                key = ("dma", s)
                if kn.get(key, 0) < self.dma_val[s]:
                    kn[key] = self.dma_val[s]
                    waits.append((key, self.dma_val[s]))
            if waits:
                self.q[e].append((waits, None, False))

    def flush(self):
        if not any(self.q[e] for e in self.ENG):
            return
        self.barrier()
        nc = self.nc
        assert self.cnt["sp"] == 0
        esem, dsem = self.esems[self.epoch], self.dsem
        q = self.q
        self.q = {e: [] for e in self.ENG}
        self.epoch += 1
        assert self.epoch < self.NEPOCH
        for e in self.ENG:
            assert self.cnt[e] < 60000, (e, self.cnt[e])
            self.cnt[e] = 0
            self.known[e] = {k: v for k, v in self.known[e].items() if isinstance(k, tuple)}

        def semof(k):
            return dsem[k[1]] if isinstance(k, tuple) else esem[k]

        def run(ename, h):
            for waits, fn, inc in q[ename]:
                for k, v in waits:
                    h.wait_ge(semof(k), v)
                if fn is None:
                    continue
                if isinstance(fn, tuple):
                    _, out, in_, s, kw = fn
                    h.dma_start(out=out, in_=in_, **kw).then_inc(dsem[s], 16)
                else:
                    ins = fn(h)
                    if inc:
                        ins.then_inc(esem[ename], 1)

        with nc.Block() as block:
            @block.tensor
            def _(h):
                run("pe", h)

            @block.scalar
            def _(h):
                run("act", h)

            @block.vector
            def _(h):
                run("dve", h)

            @block.gpsimd
            def _(h):
                run("pool", h)

            @block.sync
            def _(h):
                run("sp", h)


_UID = [0]


def _uniq(name):
    _UID[0] += 1
    return "sb%d_%s" % (_UID[0], name)


class Buf:
    __slots__ = ("h", "t")

    def __init__(self, h):
        self.h = h
        self.t = T()


class Ring:
    def __init__(self, bufs):
        self.b = bufs
        self.i = 0

    def next(self):
        b = self.b[self.i]
        self.i = (self.i + 1) % len(self.b)
        return b


def _rope_tables():
    rows = (np.arange(SEQ) // 64).astype(np.float32)
    cols = (np.arange(SEQ) % 64).astype(np.float32)

    def tab(dim):
        d = dim // 2
        half = d // 2
        inv = (np.float32(10000.0) ** (-np.arange(half, dtype=np.float32) / np.float32(half))).astype(np.float32)
        C = np.ones((dim, TT), np.float32)
        Sg = np.zeros((dim, TT), np.float32)
        for bi, pos in enumerate((rows, cols)):
            ang = (pos[:, None] * inv[None, :]).astype(np.float32)
            c = np.cos(ang).astype(np.float32).T
            s = np.sin(ang).astype(np.float32).T
            o = bi * d
            C[o:o + half, NCTX:] = c
            C[o + half:o + d, NCTX:] = c
            Sg[o:o + half, NCTX:] = -s
            Sg[o + half:o + d, NCTX:] = s
        return C, Sg

    c32, s32 = tab(32)
    c64, s64 = tab(64)
    tabs = {
        "tab_mq_c": np.concatenate([np.ones((64, TT), np.float32), c32], 0),
        "tab_mq_s": np.concatenate([np.zeros((64, TT), np.float32), s32], 0),
        "tab_kr_c": c32, "tab_kr_s": s32,
        "tab_dq_c": np.concatenate([c64, c64], 0), "tab_dq_s": np.concatenate([s64, s64], 0),
    }
    return {k: np.ascontiguousarray(v) for k, v in tabs.items()}


def _swap_idx(n, blk):
    i = np.arange(n)
    h = blk // 2
    return (i // blk) * blk + (i % blk + h) % blk


def _shared_inputs(inp):
    f = lambda a: np.ascontiguousarray(a, dtype=np.float32)
    w_in = inp["w_in"]
    o = np.cumsum([0, 256, 128, 32, 512, 512, 512, 512, 3072])
    sh = {}
    sh["w_cq"] = f(w_in[:, :, o[0]:o[1]])
    sh["w_ckv"] = f(w_in[:, :, o[1]:o[2]])
    kr = w_in[:, :, o[2]:o[3]]
    sh["w_kr"] = f(np.concatenate([kr, kr[:, :, _swap_idx(32, 16)]], -1))
    dq = w_in[:, :, o[3]:o[4]]
    sh["w_dq"] = f(np.concatenate([dq, dq[:, :, _swap_idx(512, 32)]], -1))
    dk = w_in[:, :, o[4]:o[5]]
    sh["w_dk"] = f(np.concatenate([dk, dk[:, :, _swap_idx(512, 32)]], -1))
    sh["w_dv"] = f(w_in[:, :, o[5]:o[6]])
    u = w_in[:, :, o[6]:o[7]].reshape(L, D, 32, 16)
    up = np.zeros((L, D, 32, 32), np.float32)
    up[..., :16] = u
    sh["w_u"] = f(up.reshape(L, D, 1024))
    sh["w_g"] = f(w_in[:, :, o[7]:o[8]])
    sh["w_mod"] = f(inp["w_mod"])
    sh["b_modT"] = f(inp["b_mod"].reshape(L, 48, 128).transpose(2, 0, 1))
    uq = inp["mla_w_uq"].reshape(L, 256, 8, 96)
    uqe = np.zeros((L, 256, 8, 2, 96), np.float32)
    uqe[:, :, :, 0, :] = uq
    uqe[:, :, :, 1, 64:] = uq[:, :, :, 64:][..., _swap_idx(32, 16)]
    sh["w_uq"] = f(uqe.reshape(L, 256, 8 * 2 * 96))
    ukv = inp["mla_w_ukv"].reshape(L, 128, 8, 128)
    sh["w_ukvk"] = f(ukv[:, :, :, :64].reshape(L, 128, 512))
    sh["w_ukvv"] = f(ukv[:, :, :, 64:].reshape(L, 128, 512))
    sh["q_normT"] = f(inp["mla_q_norm"].reshape(L, 2, 128).transpose(2, 0, 1))
    sh["kv_normT"] = f(inp["mla_kv_norm"].T)
    sh["mla_w_o"] = f(inp["mla_w_o"])
    sh["diff_w_o"] = f(inp["diff_w_o"])
    sh["diff_lam"] = f(inp["diff_lambda"].reshape(1, L * 256))
    sh["diff_sublnT"] = f(inp["diff_subln"].T)
    lam = np.stack([inp["s5_lambda_re"].reshape(L, 64, 64), inp["s5_lambda_im"].reshape(L, 64, 64),
                    np.broadcast_to(inp["s5_log_dt"].reshape(L, 64, 1), (L, 64, 64))], 1)
    sh["s5_lam"] = f(lam.transpose(3, 0, 1, 2))
    b = np.stack([inp["s5_b_re"], inp["s5_b_im"]], 1).reshape(L, 2, 64, 64, 16)
    sh["s5_b"] = f(b.transpose(3, 0, 1, 2, 4))
    c = np.stack([inp["s5_c_re"], inp["s5_c_im"]], 1).reshape(L, 2, 64, 16, 64)
    sh["s5_c"] = f(c.transpose(4, 0, 1, 2, 3))
    dsk = np.zeros((L, 32, 32), np.float32)
    dsk[:, :, :16] = inp["s5_d"].reshape(L, 32, 16)
    sh["s5_dT"] = f(dsk.reshape(L, 8, 128).transpose(2, 0, 1))
    glu = np.zeros((L, 32, 32, 2048), np.float32)
    glu[:, :, :16, :] = inp["s5_w_glu"].reshape(L, 32, 16, 2048)
    sh["w_glu"] = f(glu.reshape(L, 1024, 2048))
    sh["w_out"] = f(inp["w_out"])
    sh["w_up"] = f(inp["ffn_w_up"])
    sh["conv_wT"] = f(inp["ffn_conv_w"].reshape(L, 3, NFT, 128).transpose(3, 0, 1, 2))
    sh["conv_bT"] = f(inp["ffn_conv_b"].reshape(L, NFT, 128).transpose(2, 0, 1))
    sh["w_down"] = f(inp["ffn_w_down"])
    sh["fnormT"] = f(inp["final_norm"].reshape(8, 128).T)
    sh["ident"] = np.eye(128, dtype=np.float32)
    m = np.zeros((128, 128), np.float32)
    for g in range(4):
        m[32 * g:32 * g + 32, 32 * g:32 * g + 32] = 1.0
    sh["bdmask"] = m
    sh.update(_rope_tables())
    return sh


def _core_inputs(inp, b):
    xin = np.ascontiguousarray(np.concatenate([inp["ctx"][b], inp["x"][b]], 0), dtype=np.float32)
    cv = np.stack([inp["c"][b], inp["c_ctx"]], 1).astype(np.float32)
    cvT = np.ascontiguousarray(cv.reshape(8, 128, 2).transpose(1, 0, 2))
    return {"xin": xin, "cvT": cvT}


INPUT_SHAPES = None


def build(shapes, dbg=(), upto="all", layers=(0, 1)):
    nc = bass.Bass("TRN2", target_bir_lowering=False)
    IN = {k: nc.dram_tensor(k, list(s), F32, kind="ExternalInput").ap() for k, s in shapes.items()}
    out_d = nc.dram_tensor("out", [SEQ, D], F32, kind="ExternalOutput").ap()

    def dr(name, shape, dt):
        if name in dbg:
            return nc.dram_tensor(name, list(shape), dt, kind="ExternalOutput").ap()
        return nc.dram_tensor(name, list(shape), dt).ap()

    RS = [dr("RS0", [D, TT], F32), dr("RS1", [D, TT], F32)]
    MQ = dr("MQ", [8, 96, TT], BF16)
    KN = dr("KN", [4, 128, TT], BF16)
    KR = dr("KR", [32, TT], BF16)
    MV = dr("MV", [8, 128, 34, 64], BF16)
    DQ = dr("DQ", [4, 128, TT], BF16)
    DK = dr("DK", [4, 128, TT], BF16)
    DV = dr("DV", [4, 128, 34, 128], BF16)
    UT = dr("UT", [1024, TT], BF16)
    MO = dr("MO", [512, TT], BF16)
    DO = dr("DO", [512, TT], BF16)
    ZT = dr("ZT", [1024, TT], BF16)
    ATD = dr("ATD", [D, TT], BF16)
    PTD = dr("PTD", [128, TC, 8, 2, 2, 64], BF16)
    BDD = dr("BDD", [128, TC, 8, 2, 128], BF16)
    CLD = dr("CLD", [64, TC, 2, 8, 4, 2, 16], BF16)

    gstack = contextlib.ExitStack()
    with gstack:
        S = Sched(nc, gstack)
        PSB = [Buf(gstack.enter_context(nc.psum_tensor("ps%d" % i, [128, 512], F32))) for i in range(8)]
        PSR = Ring(PSB)

        def gsb(name, shape, dt=F32):
            return Buf(gstack.enter_context(nc.sbuf_tensor(_uniq(name), list(shape), dt)))

        def mm(out, lhsT, rhs, start, stop, R, W, inc=True, **kw):
            S.op("pe", lambda h: h.matmul(out, lhsT=lhsT, rhs=rhs, start=start, stop=stop, **kw), R=R, W=W, inc=inc)

        def tr(out, in_, ident, R, W, inc=True):
            S.op("pe", lambda h: h.transpose(out, in_, ident), R=R, W=W, inc=inc)

        def act(out, in_, func, R, W, bias=None, scale=None, eng="act"):
            kw = {}
            if bias is not None:
                kw["bias"] = bias
            if scale is not None:
                kw["scale"] = scale
            S.op(eng, lambda h: h.activation(out=out, in_=in_, func=func, **kw), R=R, W=W)

        def tt(out, in0, in1, op, R, W, eng="dve"):
            S.op(eng, lambda h: h.tensor_tensor(out=out, in0=in0, in1=in1, op=op), R=R, W=W)

        def ts(out, in0, s1, s2, op0, op1, R, W, eng="dve"):
            if op1 is None:
                S.op(eng, lambda h: h.tensor_scalar(out=out, in0=in0, scalar1=s1, scalar2=None, op0=op0), R=R, W=W)
            else:
                S.op(eng, lambda h: h.tensor_scalar(out=out, in0=in0, scalar1=s1, scalar2=s2, op0=op0, op1=op1), R=R, W=W)

        def stt(out, in0, scalar, in1, op0, op1, R, W):
            S.op("dve", lambda h: h.scalar_tensor_tensor(out=out, in0=in0, scalar=scalar, in1=in1, op0=op0, op1=op1), R=R, W=W)

        def cp(out, in_, R, W, eng="dve"):
            S.op(eng, lambda h: h.tensor_copy(out=out, in_=in_), R=R, W=W)

        def rcp(out, in_, R, W):
            S.op("dve", lambda h: h.reciprocal(out=out, in_=in_), R=R, W=W)

        def mset(ap, val, W, eng="pool"):
            S.op(eng, lambda h: h.memset(ap, val), W=W)

        def ld(out, in_, W, R=(), eng="sp", **kw):
            S.dma(eng, out, in_, R=R, W=W, **kw)

        def st_(out, in_, R, W=(), eng="sp", **kw):
            S.dma(eng, out, in_, R=R, W=W, **kw)

        def ldw(out, in_, W):
            S.dma("pool", out, in_, W=W)

        identf = gsb("identf", [128, 128])
        identb = gsb("identb", [128, 128], BF16)
        onesb = gsb("onesb", [128, 128], BF16)
        modv = gsb("modv", [128, L, 2, 48])
        modp = gsb("modp", [128, L, 2, 48])
        epsb = gsb("epsb", [128, 1])
        ld(identf.h[:], IN["ident"], W=[identf.t])
        cp(identb.h[:], identf.h[:], R=[identf.t], W=[identb.t])
        mset(onesb.h[:], 1.0, W=[onesb.t])
        mset(epsb.h[:], EPS, W=[epsb.t])
        TILES = [(0, NCTX)] + [(NCTX + 512 * i, 512) for i in range(8)]
        RSv = [r.rearrange("(k p) t -> p k t", p=128) for r in RS]

        with contextlib.ExitStack() as ps_:
            def sb(name, shape, dt=F32):
                return Buf(ps_.enter_context(nc.sbuf_tensor(_uniq(name), list(shape), dt)))
            cvT = sb("cvT", [128, 8, 2])
            sT = sb("sT", [128, 8, 2])
            bmT = sb("bmT", [128, L, 48])
            modrow = sb("modrow", [2, 6144])
            wmr = Ring([sb("wm%d" % i, [128, 3072]) for i in range(2)])
            ld(cvT.h[:], IN["cvT"], W=[cvT.t])
            ld(bmT.h[:], IN["b_modT"], W=[bmT.t])
            act(sT.h[:], cvT.h[:], AF.Silu, R=[cvT.t], W=[sT.t])
            for li in range(L):
                for hf in range(2):
                    banks = [PSR.next() for _ in range(6)]
                    for k in range(8):
                        wm = wmr.next()
                        ld(wm.h[:], IN["w_mod"][li, k * 128:(k + 1) * 128, hf * 3072:(hf + 1) * 3072], W=[wm.t])
                        for j in range(6):
                            mm(banks[j].h[0:2, :], sT.h[:, k, :], wm.h[:, j * 512:(j + 1) * 512], k == 0, k == 7,
                               R=[sT.t, wm.t], W=[banks[j].t])
                    for j in range(6):
                        c0 = hf * 3072 + j * 512
                        if j % 2 == 0:
                            act(modrow.h[:, c0:c0 + 512], banks[j].h[0:2, :], AF.Identity, R=[banks[j].t], W=[modrow.t])
                        else:
                            cp(modrow.h[:, c0:c0 + 512], banks[j].h[0:2, :], R=[banks[j].t], W=[modrow.t])
                pb = PSR.next()
                for c in range(48):
                    tr(pb.h[:, 2 * c:2 * c + 2], modrow.h[0:2, c * 128:(c + 1) * 128], identf.h[0:2, 0:2],
                       R=[modrow.t, identf.t], W=[pb.t])
                pv = pb.h[:, 0:96].rearrange("p (c two) -> p c two", two=2)
                for w_ in range(2):
                    tt(modv.h[:, li, w_, :], pv[:, :, w_], bmT.h[:, li, :], ALU.add, R=[pb.t, bmT.t], W=[modv.t])
            ts(modp.h[:], modv.h[:], 1.0, None, ALU.add, None, R=[modv.t], W=[modp.t])
            if "MODV" in dbg:
                mo_d = nc.dram_tensor("MODV", [128, L * 2 * 48], F32, kind="ExternalOutput").ap()
                st_(mo_d, modv.h[:].rearrange("p l w c -> p (l w c)"), R=[modv.t])
                mr_d = nc.dram_tensor("MODROW", [2, 6144], F32, kind="ExternalOutput").ap()
                st_(mr_d, modrow.h[:], R=[modrow.t])

            xlr = Ring([sb("xl%d" % i, [128, 4, D]) for i in range(2)])
            xsr = Ring([sb("xs%d" % i, [128, 8, 512]) for i in range(2)])
            for (t0, n) in TILES:
                nb = n // 128
                xl = xlr.next()
                ld(xl.h[:, 0:nb, :], IN["xin"][t0:t0 + n, :].rearrange("(j p) d -> p j d", p=128), W=[xl.t])
                xs = xsr.next()
                for k in range(8):
                    pb = PSR.next()
                    for j in range(nb):
                        tr(pb.h[:, j * 128:(j + 1) * 128], xl.h[:, j, k * 128:(k + 1) * 128], identf.h[:],
                           R=[xl.t, identf.t], W=[pb.t])
                    if k % 2 == 0:
                        act(xs.h[:, k, 0:n], pb.h[:, 0:n], AF.Identity, R=[pb.t], W=[xs.t])
                    else:
                        cp(xs.h[:, k, 0:n], pb.h[:, 0:n], R=[pb.t], W=[xs.t])
                st_(RSv[0][:, :, t0:t0 + n], xs.h[:, :, 0:n], R=[xs.t])
            S.flush()

        def mvec(li, which, chunk):
            return modv.h[:, li, which, chunk:chunk + 1]

        def mvecp(li, which, chunk):
            return modp.h[:, li, which, chunk:chunk + 1]

        def modulate(rsr, xs, n, li, which, sh0, sc0, outf, tmpr, sqr):
            sq = sqr.next()
            act(sq.h[:, :, 0:n], xs.h[:, :, 0:n], AF.Square, R=[xs.t], W=[sq.t])
            pb = PSR.next()
            for k in range(8):
                mm(pb.h[:, 0:n], onesb.h[:], sq.h[:, k, 0:n], k == 0, k == 7, R=[onesb.t, sq.t], W=[pb.t], inc=(k == 7))
            rt = tmpr.next()
            act(rt.h[:, 0:n], pb.h[:, 0:n], AF.Sqrt, R=[pb.t, epsb.t], W=[rt.t], bias=epsb.h[:, 0:1], scale=1.0 / D)
            rs = rsr.next()
            rcp(rs.h[:, 0:n], rt.h[:, 0:n], R=[rt.t], W=[rs.t])
            for k in range(8):
                tm = tmpr.next()
                stt(tm.h[:, 0:n], xs.h[:, k, 0:n], mvecp(li, which, sc0 + k), rs.h[:, 0:n], ALU.mult, ALU.mult,
                    R=[xs.t, rs.t, modp.t], W=[tm.t])
                oap, otile = outf(k)
                act(oap, tm.h[:, 0:n], AF.Identity, R=[tm.t, modv.t], W=[otile], bias=mvec(li, which, sh0 + k))

        def phase_proj(li):
            need_ctx = li < L - 1
            with contextlib.ExitStack() as ps_:
                def sb(name, shape, dt=F32):
                    return Buf(ps_.enter_context(nc.sbuf_tensor(_uniq(name), list(shape), dt)))
                AT = sb("AT", [128, 8, TT], BF16)
                xsr = Ring([sb("pxs%d" % i, [128, 8, 512]) for i in range(2)])
                sqr = Ring([sb("psq%d" % i, [128, 8, 512], BF16) for i in range(1)])
                tmpr = Ring([sb("ptm%d" % i, [128, 512]) for i in range(4)])
                rsr = Ring([sb("prs%d" % i, [128, 512]) for i in range(2)])
                for (t0, n) in TILES:
                    which = 1 if t0 == 0 else 0
                    xs = xsr.next()
                    ld(xs.h[:, :, 0:n], RSv[0][:, :, t0:t0 + n], W=[xs.t])
                    modulate(rsr, xs, n, li, which, 0, 8, lambda k: (AT.h[:, k, t0:t0 + n], AT.t), tmpr, sqr)
                st_(ATD.rearrange("(k p) t -> p k t", p=128), AT.h[:], R=[AT.t])

                def wload(name, key, cols, kch=8):
                    w = sb(name, [128, kch, cols], BF16)
                    src = IN[key][li].rearrange("(k p) c -> p k c", p=128)
                    step = 1024
                    for c0 in range(0, cols, step):
                        c1 = min(cols, c0 + step)
                        ldw(w.h[:, :, c0:c1], src[:, :, c0:c1], W=[w.t])
                    return w
                w_cq = wload("w_cq", "w_cq", 256)
                w_uq = wload("w_uq", "w_uq", 1536, kch=2)
                w_ckv = wload("w_ckv", "w_ckv", 128)
                w_kr = wload("w_kr", "w_kr", 64)
                w_ukvk = sb("w_ukvk", [128, 512], BF16)
                w_ukvv = sb("w_ukvv", [128, 512], BF16)
                ldw(w_ukvk.h[:], IN["w_ukvk"][li], W=[w_ukvk.t])
                ldw(w_ukvv.h[:], IN["w_ukvv"][li], W=[w_ukvv.t])
                qn = sb("qn", [128, L, 2])
                kvn = sb("kvn", [128, L])
                ld(qn.h[:], IN["q_normT"], W=[qn.t])
                ld(kvn.h[:], IN["kv_normT"], W=[kvn.t])
                tabr = Ring([sb("tab%d" % i, [128, 512]) for i in range(4)])
                sqb = Ring([sb("sqb%d" % i, [128, 2, 512], BF16) for i in range(2)])
                nrm = Ring([sb("nrm%d" % i, [128, 2, 512], BF16) for i in range(2)])
                outb = Ring([sb("outb%d" % i, [128, 512], BF16) for i in range(4)])
                evi = [0]

                def evac(out, in_, R, W):
                    evi[0] += 1
                    if evi[0] % 2:
                        act(out, in_, AF.Identity, R=R, W=W)
                    else:
                        cp(out, in_, R=R, W=W)

                def rope_combine(p1, p2, tc, tsn, rows, n, dst_ap):
                    a = tmpr.next()
                    tt(a.h[0:rows, 0:n], p1.h[0:rows, 0:n], tc.h[0:rows, 0:n], ALU.mult, R=[p1.t, tc.t], W=[a.t])
                    b = tmpr.next()
                    tt(b.h[0:rows, 0:n], p2.h[0:rows, 0:n], tsn.h[0:rows, 0:n], ALU.mult, R=[p2.t, tsn.t], W=[b.t])
                    o = outb.next()
                    tt(o.h[0:rows, 0:n], a.h[0:rows, 0:n], b.h[0:rows, 0:n], ALU.add, R=[a.t, b.t], W=[o.t], eng="pool")
                    st_(dst_ap, o.h[0:rows, 0:n], R=[o.t])

                def lownorm(pbs, nchunk, dim, gains, n):
                    sq = sqb.next()
                    for j in range(nchunk):
                        act(sq.h[:, j, 0:n], pbs[j].h[:, 0:n], AF.Square, R=[pbs[j].t], W=[sq.t])
                    pc = PSR.next()
                    for j in range(nchunk):
                        mm(pc.h[:, 0:n], onesb.h[:], sq.h[:, j, 0:n], j == 0, j == nchunk - 1, R=[onesb.t, sq.t], W=[pc.t],
                           inc=(j == nchunk - 1))
                    rt = tmpr.next()
                    act(rt.h[:, 0:n], pc.h[:, 0:n], AF.Sqrt, R=[pc.t, epsb.t], W=[rt.t], bias=epsb.h[:, 0:1], scale=1.0 / dim)
                    rs = tmpr.next()
                    rcp(rs.h[:, 0:n], rt.h[:, 0:n], R=[rt.t], W=[rs.t])
                    o = nrm.next()
                    for j in range(nchunk):
                        stt(o.h[:, j, 0:n], pbs[j].h[:, 0:n], gains[j], rs.h[:, 0:n], ALU.mult, ALU.mult,
                            R=[pbs[j].t, rs.t, qn.t, kvn.t], W=[o.t])
                    return o

                for (t0, n) in TILES:
                    if t0 == 0 and not need_ctx:
                        continue
                    tc = tabr.next(); tsn = tabr.next()
                    ld(tc.h[0:96, 0:n], IN["tab_mq_c"][:, t0:t0 + n], W=[tc.t])
                    ld(tsn.h[0:96, 0:n], IN["tab_mq_s"][:, t0:t0 + n], W=[tsn.t])
                    pbs = [PSR.next(), PSR.next()]
                    for j in range(2):
                        for k in range(8):
                            mm(pbs[j].h[:, 0:n], w_cq.h[:, k, j * 128:(j + 1) * 128], AT.h[:, k, t0:t0 + n], k == 0, k == 7,
                               R=[w_cq.t, AT.t], W=[pbs[j].t], inc=(k == 7))
                    cqn = lownorm(pbs, 2, 256, [qn.h[:, li, 0:1], qn.h[:, li, 1:2]], n)
                    for h in range(8):
                        p1 = PSR.next(); p2 = PSR.next()
                        for v, pp in ((0, p1), (1, p2)):
                            c0 = (h * 2 + v) * 96
                            for j in range(2):
                                mm(pp.h[0:96, 0:n], w_uq.h[:, j, c0:c0 + 96], cqn.h[:, j, 0:n], j == 0, j == 1,
                                   R=[w_uq.t, cqn.t], W=[pp.t], inc=(j == 1))
                        rope_combine(p1, p2, tc, tsn, 96, n, MQ[h, :, t0:t0 + n])
                for (t0, n) in TILES:
                    tc = tabr.next(); tsn = tabr.next()
                    ld(tc.h[0:32, 0:n], IN["tab_kr_c"][:, t0:t0 + n], W=[tc.t])
                    ld(tsn.h[0:32, 0:n], IN["tab_kr_s"][:, t0:t0 + n], W=[tsn.t])
                    pa = PSR.next()
                    for k in range(8):
                        mm(pa.h[:, 0:n], w_ckv.h[:, k, :], AT.h[:, k, t0:t0 + n], k == 0, k == 7, R=[w_ckv.t, AT.t], W=[pa.t],
                           inc=(k == 7))
                    ckvn = lownorm([pa], 1, 128, [kvn.h[:, li:li + 1]], n)
                    for pr in range(4):
                        pk = PSR.next()
                        mm(pk.h[:, 0:n], w_ukvk.h[:, pr * 128:(pr + 1) * 128], ckvn.h[:, 0, 0:n], True, True,
                           R=[w_ukvk.t, ckvn.t], W=[pk.t])
                        o = outb.next()
                        evac(o.h[:, 0:n], pk.h[:, 0:n], R=[pk.t], W=[o.t])
                        st_(KN[pr, :, t0:t0 + n], o.h[:, 0:n], R=[o.t])
                    for j in range(n // 128):
                        pv = PSR.next()
                        mm(pv.h[:, :], ckvn.h[:, 0, j * 128:(j + 1) * 128], w_ukvv.h[:, :], True, True,
                           R=[w_ukvv.t, ckvn.t], W=[pv.t])
                        o = outb.next()
                        evac(o.h[:, :], pv.h[:, :], R=[pv.t], W=[o.t])
                        st_(MV[:, :, (t0 // 128) + j, :].rearrange("h p d -> p h d"),
                            o.h[:, :].rearrange("p (h d) -> p h d", h=8), R=[o.t])
                    p1 = PSR.next(); p2 = PSR.next()
                    for v, pp in ((0, p1), (1, p2)):
                        for k in range(8):
                            mm(pp.h[0:32, 0:n], w_kr.h[:, k, v * 32:(v + 1) * 32], AT.h[:, k, t0:t0 + n], k == 0, k == 7,
                               R=[w_kr.t, AT.t], W=[pp.t], inc=(k == 7))
                    rope_combine(p1, p2, tc, tsn, 32, n, KR[:, t0:t0 + n])
                S.flush()
                wq = sb("w_dqk", [128, 8, 1024], BF16)
                for kind, key, dst in (("q", "w_dq", DQ), ("k", "w_dk", DK)):
                    src = IN[key][li].rearrange("(k p) c -> p k c", p=128)
                    ldw(wq.h[:], src, W=[wq.t])
                    for (t0, n) in TILES:
                        if kind == "q" and t0 == 0 and not need_ctx:
                            continue
                        tc = tabr.next(); tsn = tabr.next()
                        ld(tc.h[:, 0:n], IN["tab_dq_c"][:, t0:t0 + n], W=[tc.t])
                        ld(tsn.h[:, 0:n], IN["tab_dq_s"][:, t0:t0 + n], W=[tsn.t])
                        for h in range(4):
                            p1 = PSR.next(); p2 = PSR.next()
                            for v, pp in ((0, p1), (1, p2)):
                                c0 = v * 512 + h * 128
                                for k in range(8):
                                    mm(pp.h[:, 0:n], wq.h[:, k, c0:c0 + 128], AT.h[:, k, t0:t0 + n], k == 0, k == 7,
                                       R=[wq.t, AT.t], W=[pp.t], inc=(k == 7))
                            rope_combine(p1, p2, tc, tsn, 128, n, dst[h, :, t0:t0 + n])
                ldw(wq.h[:, :, 0:512], IN["w_dv"][li].rearrange("(k p) c -> p k c", p=128), W=[wq.t])
                for j in range(TT // 128):
                    pv = PSR.next()
                    for k in range(8):
                        mm(pv.h[:, :], AT.h[:, k, j * 128:(j + 1) * 128], wq.h[:, k, 0:512], k == 0, k == 7,
                           R=[wq.t, AT.t], W=[pv.t], inc=(k == 7))
                    o = outb.next()
                    evac(o.h[:, :], pv.h[:, :], R=[pv.t], W=[o.t])
                    st_(DV[:, :, j, :].rearrange("h p d -> p h d"), o.h[:, :].rearrange("p (h d) -> p h d", h=4), R=[o.t])
                ldw(wq.h[:], IN["w_u"][li].rearrange("(k p) c -> p k c", p=128), W=[wq.t])
                for (t0, n) in TILES:
                    for m in range(8):
                        pu = PSR.next()
                        for k in range(8):
                            mm(pu.h[:, 0:n], wq.h[:, k, m * 128:(m + 1) * 128], AT.h[:, k, t0:t0 + n], k == 0, k == 7,
                               R=[wq.t, AT.t], W=[pu.t], inc=(k == 7))
                        o = outb.next()
                        evac(o.h[:, 0:n], pu.h[:, 0:n], R=[pu.t], W=[o.t])
                        st_(UT[m * 128:(m + 1) * 128, t0:t0 + n], o.h[:, 0:n], R=[o.t])
                S.flush()

        def phase_mla(li):
            need_ctx = li < L - 1
            scale = 96.0 ** -0.5
            with contextlib.ExitStack() as ps_:
                def sb(name, shape, dt=F32):
                    return Buf(ps_.enter_context(nc.sbuf_tensor(_uniq(name), list(shape), dt)))
                ktr = Ring([sb("mkt%d" % i, [96, TT], BF16) for i in range(2)])
                vhr = Ring([sb("mvh%d" % i, [128, 34, 128], BF16) for i in range(2)])
                qtr = Ring([sb("mqt%d" % i, [96, 512], BF16) for i in range(2)])
                ptr_ = Ring([sb("mpt%d" % i, [128, 512], BF16) for i in range(4)])
                rdr = Ring([sb("mrd%d" % i, [128, 512]) for i in range(2)])
                rd2r = Ring([sb("mrd2%d" % i, [64, 512]) for i in range(2)])
                obr = Ring([sb("mob%d" % i, [64, 512], BF16) for i in range(2)])
                for b_ in vhr.b:
                    mset(b_.h[:, :, 64:128], 1.0, W=[b_.t])
                accs = Ring(PSB[0:2])
                scs = Ring(PSB[2:8])
                for h in range(8):
                    kt = ktr.next(); vh = vhr.next()
                    ld(kt.h[0:64, :], KN[h // 2, (h % 2) * 64:(h % 2) * 64 + 64, :], W=[kt.t])
                    ld(kt.h[64:96, :], KR[:, :], W=[kt.t])
                    ld(vh.h[:, :, 0:64], MV[h], W=[vh.t])
                    for (t0, n) in TILES:
                        if t0 == 0:
                            if not need_ctx:
                                continue
                            nk = 2
                        else:
                            nk = 34
                        qt = qtr.next()
                        ld(qt.h[:, 0:n], MQ[h, :, t0:t0 + n], W=[qt.t])
                        po = accs.next()
                        for kt_i in range(nk):
                            sc = scs.next()
                            mm(sc.h[:, 0:n], kt.h[:, kt_i * 128:(kt_i + 1) * 128], qt.h[:, 0:n], True, True,
                               R=[kt.t, qt.t], W=[sc.t])
                            pt = ptr_.next()
                            act(pt.h[:, 0:n], sc.h[:, 0:n], AF.Exp, R=[sc.t], W=[pt.t], scale=scale)
                            mm(po.h[:, 0:n], vh.h[:, kt_i, :], pt.h[:, 0:n], kt_i == 0, kt_i == nk - 1,
                               R=[vh.t, pt.t], W=[po.t], inc=(kt_i == nk - 1))
                        rd = rdr.next()
                        rcp(rd.h[64:128, 0:n], po.h[64:128, 0:n], R=[po.t], W=[rd.t])
                        rd2 = rd2r.next()
                        cp(rd2.h[0:64, 0:n], rd.h[64:128, 0:n], R=[rd.t], W=[rd2.t])
                        ob = obr.next()
                        tt(ob.h[:, 0:n], po.h[0:64, 0:n], rd2.h[:, 0:n], ALU.mult, R=[po.t, rd2.t], W=[ob.t])
                        st_(MO[h * 64:(h + 1) * 64, t0:t0 + n], ob.h[:, 0:n], R=[ob.t])
                S.flush()

        def phase_diff(li):
            need_ctx = li < L - 1
            lam_init = 0.8 - 0.6 * math.exp(-0.3 * li)
            scale = 64.0 ** -0.5
            with contextlib.ExitStack() as ps_:
                def sb(name, shape, dt=F32):
                    return Buf(ps_.enter_context(nc.sbuf_tensor(_uniq(name), list(shape), dt)))
                dl = sb("dl", [1, L * 256])
                lt = sb("lt", [1, 128])
                ls = sb("ls", [1, 8])
                onesf = sb("onesf", [1, 128])
                nlam = sb("nlam", [128, 1])
                sub = sb("sub", [128, L])
                subs = sb("subs", [128, 1])
                ld(dl.h[:], IN["diff_lam"], W=[dl.t])
                ld(sub.h[:], IN["diff_sublnT"], W=[sub.t])
                mset(onesf.h[:], 1.0, W=[onesf.t])
                b0 = li * 256
                tt(lt.h[:, 0:64], dl.h[:, b0:b0 + 64], dl.h[:, b0 + 64:b0 + 128], ALU.mult, R=[dl.t], W=[lt.t])
                tt(lt.h[:, 64:128], dl.h[:, b0 + 128:b0 + 192], dl.h[:, b0 + 192:b0 + 256], ALU.mult, R=[dl.t], W=[lt.t])
                S.op("dve", lambda h: h.tensor_reduce(out=ls.h[:, 0:2], in_=lt.h[:, :].rearrange("p (a b) -> p a b", a=2),
                                                      axis=mybir.AxisListType.X, op=ALU.add), R=[lt.t], W=[ls.t])
                act(ls.h[:, 2:4], ls.h[:, 0:2], AF.Exp, R=[ls.t], W=[ls.t])
                tt(ls.h[:, 4:5], ls.h[:, 3:4], ls.h[:, 2:3], ALU.subtract, R=[ls.t], W=[ls.t])
                ts(ls.h[:, 5:6], ls.h[:, 4:5], -lam_init, None, ALU.add, None, R=[ls.t], W=[ls.t])
                pb = PSR.next()
                mm(pb.h[:, 0:1], onesf.h[0:1, :], ls.h[0:1, 5:6], True, True, R=[onesf.t, ls.t], W=[pb.t])
                cp(nlam.h[:], pb.h[:, 0:1], R=[pb.t], W=[nlam.t])
                ts(subs.h[:], sub.h[:, li:li + 1], 1.0 - lam_init, None, ALU.mult, None, R=[sub.t], W=[subs.t])

                ktr = Ring([sb("dkt%d" % i, [128, TT], BF16) for i in range(2)])
                vhr = Ring([sb("dvh%d" % i, [128, 34, 128], BF16) for i in range(2)])
                qtr = Ring([sb("dqt%d" % i, [128, 512], BF16) for i in range(2)])
                ptr_ = Ring([sb("dpt%d" % i, [128, 512], BF16) for i in range(4)])
                f32r = Ring([sb("df%d" % i, [128, 512]) for i in range(6)])
                sqr = Ring([sb("dsq%d" % i, [128, 512], BF16) for i in range(2)])
                obr = Ring([sb("dob%d" % i, [128, 512], BF16) for i in range(2)])
                scs = Ring(PSB[4:8])
                for h in range(4):
                    kt = ktr.next(); vh = vhr.next()
                    ld(kt.h[:, :], DK[h], W=[kt.t])
                    ld(vh.h[:, :, :], DV[h], W=[vh.t])
                    for (t0, n) in TILES:
                        if t0 == 0:
                            if not need_ctx:
                                continue
                            nk = 2
                        else:
                            nk = 34
                        qt = qtr.next()
                        ld(qt.h[:, 0:n], DQ[h, :, t0:t0 + n], W=[qt.t])
                        o1, d1, o2, d2 = PSB[0], PSB[1], PSB[2], PSB[3]
                        for kt_i in range(nk):
                            first, last = kt_i == 0, kt_i == nk - 1
                            for v, (oo, dd) in enumerate(((o1, d1), (o2, d2))):
                                r0 = v * 64
                                sc = scs.next()
                                mm(sc.h[:, 0:n], kt.h[r0:r0 + 64, kt_i * 128:(kt_i + 1) * 128], qt.h[r0:r0 + 64, 0:n], True, True,
                                   R=[kt.t, qt.t], W=[sc.t])
                                pt = ptr_.next()
                                act(pt.h[:, 0:n], sc.h[:, 0:n], AF.Exp, R=[sc.t], W=[pt.t], scale=scale)
                                mm(oo.h[:, 0:n], vh.h[:, kt_i, :], pt.h[:, 0:n], first, last, R=[vh.t, pt.t], W=[oo.t], inc=last)
                                mm(dd.h[:, 0:n], onesb.h[:], pt.h[:, 0:n], first, last, R=[onesb.t, pt.t], W=[dd.t], inc=last)
                        r1 = f32r.next(); r2 = f32r.next(); a1 = f32r.next(); a2 = f32r.next(); o = f32r.next()
                        rcp(r1.h[:, 0:n], d1.h[:, 0:n], R=[d1.t], W=[r1.t])
                        rcp(r2.h[:, 0:n], d2.h[:, 0:n], R=[d2.t], W=[r2.t])
                        tt(a1.h[:, 0:n], o1.h[:, 0:n], r1.h[:, 0:n], ALU.mult, R=[o1.t, r1.t], W=[a1.t])
                        stt(a2.h[:, 0:n], o2.h[:, 0:n], nlam.h[:, 0:1], r2.h[:, 0:n], ALU.mult, ALU.mult,
                            R=[o2.t, r2.t, nlam.t], W=[a2.t])
                        tt(o.h[:, 0:n], a1.h[:, 0:n], a2.h[:, 0:n], ALU.add, R=[a1.t, a2.t], W=[o.t], eng="pool")
                        sq = sqr.next()
                        act(sq.h[:, 0:n], o.h[:, 0:n], AF.Square, R=[o.t], W=[sq.t])
                        pc = scs.next()
                        mm(pc.h[:, 0:n], onesb.h[:], sq.h[:, 0:n], True, True, R=[onesb.t, sq.t], W=[pc.t])
                        rt = f32r.next()
                        act(rt.h[:, 0:n], pc.h[:, 0:n], AF.Sqrt, R=[pc.t, epsb.t], W=[rt.t], bias=epsb.h[:, 0:1], scale=1.0 / 128)
                        rcp(r1.h[:, 0:n], rt.h[:, 0:n], R=[rt.t], W=[r1.t])
                        ob = obr.next()
                        stt(ob.h[:, 0:n], o.h[:, 0:n], subs.h[:, 0:1], r1.h[:, 0:n], ALU.mult, ALU.mult,
                            R=[o.t, r1.t, subs.t], W=[ob.t])
                        st_(DO[h * 128:(h + 1) * 128, t0:t0 + n], ob.h[:, 0:n], R=[ob.t])
                S.flush()

        def phase_s5(li):
            need_ctx = li < L - 1
            with contextlib.ExitStack() as ps_:
                def sb(name, shape, dt=F32):
                    return Buf(ps_.enter_context(nc.sbuf_tensor(_uniq(name), list(shape), dt)))
                AA = sb("AA", [64, 2, 2, 2, 32])
                with contextlib.ExitStack() as pp_:
                    def sp(name, shape, dt=F32):
                        return Buf(pp_.enter_context(nc.sbuf_tensor(_uniq(name), list(shape), dt)))
                    lam = sp("lam", [64, L, 3, 64])
                    Bc = sp("Bc", [64, 2, 64, 16])
                    Cc = sp("Cc", [64, 2, 64, 16])
                    ld(lam.h[:], IN["s5_lam"], W=[lam.t])
                    ld(Bc.h[:], IN["s5_b"][:, li], W=[Bc.t])
                    ld(Cc.h[:], IN["s5_c"][:, li], W=[Cc.t])
                    mask = sp("mask", [128, 128])
                    ld(mask.h[:], IN["bdmask"], W=[mask.t])
                    sc_ = sp("s5sc", [64, 24, 64])
                    sct = sc_.t
                    R_ = lambda i: sc_.h[:, i, :]
                    RE, DT, A_, TH, MAG, SN, CS, L1R, L1I, NR, DEN, CR, CI, T0, T1, T2, T3 = range(17)

                    def v(out, in0, in1, op):
                        tt(out, in0, in1, op, R=[sct, lam.t], W=[sct])

                    def vs(out, in0, s1, s2, op0, op1=None):
                        ts(out, in0, s1, s2, op0, op1, R=[sct, lam.t], W=[sct])

                    vs(R_(RE), lam.h[:, li, 0, :], -1e-4, None, ALU.min)
                    act(R_(DT), lam.h[:, li, 2, :], AF.Exp, R=[lam.t], W=[sct])
                    v(R_(A_), R_(RE), R_(DT), ALU.mult)
                    v(R_(TH), lam.h[:, li, 1, :], R_(DT), ALU.mult)
                    act(R_(MAG), R_(A_), AF.Exp, R=[sct], W=[sct])
                    ti = sp("s5ti", [64, 64], I32)

                    def sin_of(dst, src, shift):
                        vs(R_(T0), src, 1.0 / (2 * PI), shift / (2 * PI), ALU.mult, ALU.add)
                        cp(ti.h[:], R_(T0), R=[sct], W=[ti.t])
                        cp(R_(T1), ti.h[:], R=[ti.t], W=[sct])
                        v(R_(T2), R_(T0), R_(T1), ALU.subtract)
                        vs(R_(T2), R_(T2), 2 * PI, None, ALU.mult)
                        vs(R_(T2), R_(T2), 3.1415925, -3.1415925, ALU.min, ALU.max)
                        act(dst, R_(T2), AF.Sin, R=[sct], W=[sct])

                    sin_of(R_(SN), R_(TH), 0.0)
                    sin_of(R_(CS), R_(TH), PI / 2)
                    v(R_(L1R), R_(MAG), R_(CS), ALU.mult)
                    v(R_(L1I), R_(MAG), R_(SN), ALU.mult)
                    vs(R_(NR), R_(L1R), -1.0, None, ALU.add)
                    v(R_(T0), R_(RE), R_(RE), ALU.mult)
                    v(R_(T1), lam.h[:, li, 1, :], lam.h[:, li, 1, :], ALU.mult)
                    v(R_(DEN), R_(T0), R_(T1), ALU.add)
                    rcp(R_(DEN), R_(DEN), R=[sct], W=[sct])
                    v(R_(T0), R_(NR), R_(RE), ALU.mult)
                    v(R_(T1), R_(L1I), lam.h[:, li, 1, :], ALU.mult)
                    v(R_(T0), R_(T0), R_(T1), ALU.add)
                    v(R_(CR), R_(T0), R_(DEN), ALU.mult)
                    v(R_(T0), R_(L1I), R_(RE), ALU.mult)
                    v(R_(T1), R_(NR), lam.h[:, li, 1, :], ALU.mult)
                    v(R_(T0), R_(T0), R_(T1), ALU.subtract)
                    v(R_(CI), R_(T0), R_(DEN), ALU.mult)
                    LP = sp("LP", [64, TC + 1, 2, 64])
                    mset(LP.h[:, 0, 0, :], 1.0, W=[LP.t])
                    mset(LP.h[:, 0, 1, :], 0.0, W=[LP.t])
                    for k in range(1, TC + 1):
                        tt(R_(T0), LP.h[:, k - 1, 0, :], R_(L1R), ALU.mult, R=[LP.t, sct], W=[sct])
                        tt(R_(T1), LP.h[:, k - 1, 1, :], R_(L1I), ALU.mult, R=[LP.t, sct], W=[sct])
                        tt(LP.h[:, k, 0, :], R_(T0), R_(T1), ALU.subtract, R=[sct], W=[LP.t])
                        tt(R_(T2), LP.h[:, k - 1, 0, :], R_(L1I), ALU.mult, R=[LP.t, sct], W=[sct])
                        tt(R_(T3), LP.h[:, k - 1, 1, :], R_(L1R), ALU.mult, R=[LP.t, sct], W=[sct])
                        tt(LP.h[:, k, 1, :], R_(T2), R_(T3), ALU.add, R=[sct], W=[LP.t])
                    for d in range(2):
                        g0 = d * 32
                        for r in range(2):
                            cp(AA.h[:, d, 0, r, :], LP.h[:, TC, 0, g0:g0 + 32], R=[LP.t], W=[AA.t])
                        ts(AA.h[:, d, 1, 0, :], LP.h[:, TC, 1, g0:g0 + 32], -1.0, None, ALU.mult, None, R=[LP.t], W=[AA.t])
                        cp(AA.h[:, d, 1, 1, :], LP.h[:, TC, 1, g0:g0 + 32], R=[LP.t], W=[AA.t])
                    BB = sp("BB", [64, 2, 64, 16])
                    W1 = sp("W1", [64, 64, 16]); W2 = sp("W2", [64, 64, 16])
                    bc = lambda ap: ap.unsqueeze(2).to_broadcast([64, 64, 16])

                    def cmul(dst_r, dst_i, ar, ai, br, bi, Rr, Ww, neg_i=False):
                        tt(W1.h[:], ar, br, ALU.mult, R=Rr, W=[W1.t])
                        tt(W2.h[:], ai, bi, ALU.mult, R=Rr, W=[W2.t])
                        tt(dst_r, W1.h[:], W2.h[:], ALU.subtract, R=[W1.t, W2.t], W=Ww, eng="pool")
                        tt(W1.h[:], ar, bi, ALU.mult, R=Rr, W=[W1.t])
                        tt(W2.h[:], ai, br, ALU.mult, R=Rr, W=[W2.t])
                        if neg_i:
                            stt(dst_i, W1.h[:], -1.0, W2.h[:], ALU.mult, ALU.subtract, R=[W1.t, W2.t], W=Ww)
                        else:
                            tt(dst_i, W1.h[:], W2.h[:], ALU.add, R=[W1.t, W2.t], W=Ww, eng="pool")

                    cmul(BB.h[:, 0], BB.h[:, 1], bc(R_(CR)), bc(R_(CI)), Bc.h[:, 0], Bc.h[:, 1], [sct, Bc.t, BB.t], [BB.t])
                    Cp = sp("Cp", [64, 2, 64, 32])
                    mset(Cp.h[:], 0.0, W=[Cp.t])
                    cp(Cp.h[:, 0, :, 0:16], Cc.h[:, 0], R=[Cc.t], W=[Cp.t])
                    ts(Cp.h[:, 1, :, 0:16], Cc.h[:, 1], -1.0, None, ALU.mult, None, R=[Cc.t], W=[Cp.t])
                    Ppr = Ring([sp("Pp%d" % i, [64, 2, 64, 32]) for i in range(1)])
                    for b_ in Ppr.b:
                        mset(b_.h[:], 0.0, W=[b_.t])
                    PTs = Ring([sp("PTs%d" % i, [128, 8, 2, 2, 64], BF16) for i in range(2)])
                    BDs = Ring([sp("BDs%d" % i, [128, 8, 2, 128], BF16) for i in range(2)])
                    CLs = Ring([sp("CLs%d" % i, [64, 64, 2, 16], BF16) for i in range(2)])
                    for k in range(TC):
                        Pp = Ppr.next()
                        cmul(Pp.h[:, 0, :, 0:16], Pp.h[:, 1, :, 0:16], bc(LP.h[:, k, 0, :]), bc(LP.h[:, k, 1, :]),
                             BB.h[:, 0], BB.h[:, 1], [LP.t, BB.t, Pp.t], [Pp.t])
                        pts = PTs.next(); bds = BDs.next()
                        for d in range(2):
                            for tl in range(8):
                                g0 = d * 32 + tl * 4
                                if tl % 4 == 0:
                                    pbt = PSR.next()
                                for r in range(2):
                                    c0 = ((tl % 4) * 2 + r) * 64
                                    tr(pbt.h[:, c0:c0 + 64], Pp.h[:, r, g0:g0 + 4, :].rearrange("p a b -> p (a b)"), identf.h[0:64, 0:64],
                                       R=[Pp.t, identf.t], W=[pbt.t], inc=(tl % 4 == 3 and r == 1))
                                if tl % 4 == 3:
                                    t4 = tl - 3
                                    src = pbt.h[:, :].rearrange("p (a r n) -> p a r n", a=4, r=2)
                                    if d == 0:
                                        act(pts.h[:, t4:t4 + 4, d, :, :], src, AF.Identity, R=[pbt.t], W=[pts.t])
                                    else:
                                        cp(pts.h[:, t4:t4 + 4, d, :, :], src, R=[pbt.t], W=[pts.t])
                                pbd = PSR.next()
                                for r in range(2):
                                    mm(pbd.h[:, 0:128], Pp.h[:, r, g0:g0 + 4, :].rearrange("p a b -> p (a b)"),
                                       Cp.h[:, r, g0:g0 + 4, :].rearrange("p a b -> p (a b)"), r == 0, r == 1,
                                       R=[Pp.t, Cp.t], W=[pbd.t], inc=(r == 1))
                                tt(bds.h[:, tl, d, :], pbd.h[:, 0:128], mask.h[:], ALU.mult, R=[pbd.t, mask.t], W=[bds.t])
                        st_(PTD[:, k].rearrange("p t d r n -> p (t d r n)"),
                            pts.h[:].rearrange("p t d r n -> p (t d r n)"), R=[pts.t])
                        st_(BDD[:, k].rearrange("p t d m -> p (t d m)"), bds.h[:].rearrange("p t d m -> p (t d m)"), R=[bds.t])
                        cl = CLs.next()
                        cmul(cl.h[:, :, 0, :], cl.h[:, :, 1, :], bc(LP.h[:, k + 1, 0, :]), bc(LP.h[:, k + 1, 1, :]),
                             Cc.h[:, 0], Cc.h[:, 1], [LP.t, Cc.t, cl.t], [cl.t], neg_i=True)
                        st_(CLD[:, k].rearrange("n d t g r o -> n (d t g r o)"),
                            cl.h[:].rearrange("n a r o -> n (a r o)"), R=[cl.t])
                    S.flush()
                Ssb = sb("Ssb", [64, 2, 64, NCH], BF16)
                SsT = [T(), T()]
                utr = Ring([sb("ut%d" % i, [128, TT], BF16) for i in range(2)])
                ptl = Ring([sb("ptl%d" % i, [128, TC, 2, 2, 64], BF16) for i in range(2)])
                evi = [0]
                for tl in range(8):
                    ut = utr.next(); pt = ptl.next()
                    ld(ut.h[:], UT[tl * 128:(tl + 1) * 128, :], W=[ut.t])
                    ld(pt.h[:].rearrange("p k d r n -> p k (d r n)"), PTD[:, :, tl].rearrange("p k d r n -> p k (d r n)"), W=[pt.t])
                    uv = ut.h[:, :].rearrange("p (c j) -> p c j", j=TC)
                    for d in range(2):
                        for gl in range(4):
                            for r in range(2):
                                pb = PSR.next()
                                for s in range(TC):
                                    kk = (TC - 1 - s) if d == 0 else s
                                    mm(pb.h[0:64, 0:NCH], pt.h[32 * gl:32 * gl + 32, kk, d, r, :], uv[32 * gl:32 * gl + 32, :, s],
                                       s == 0, s == TC - 1, R=[pt.t, ut.t], W=[pb.t], inc=(s == TC - 1), tile_position=(32 * gl, 0))
                                evi[0] += 1
                                dst = Ssb.h[:, r, d * 32 + tl * 4 + gl, :]
                                if evi[0] % 2:
                                    act(dst, pb.h[0:64, 0:NCH], AF.Identity, R=[pb.t], W=[SsT[d]])
                                else:
                                    cp(dst, pb.h[0:64, 0:NCH], R=[pb.t], W=[SsT[d]])
                X = [sb("X%d" % d, [64, 2, 32]) for d in range(2)]
                Tt = [sb("Tt%d" % d, [64, 2, 32]) for d in range(2)]
                Uu = [sb("Uu%d" % d, [64, 2, 32]) for d in range(2)]
                fwd_order = list(range(NCH))
                rev_order = list(range(15, -1, -1)) + list(range(NCH - 1, 15, -1))
                for d in range(2):
                    mset(X[d].h[:], 0.0, W=[X[d].t])
                for i in range(NCH):
                    for d, eng in ((0, "dve"), (1, "pool")):
                        c = fwd_order[i] if d == 0 else rev_order[i]
                        x, t_, u_ = X[d], Tt[d], Uu[d]
                        tt(t_.h[:], x.h[:], AA.h[:, d, 0], ALU.mult, R=[x.t, AA.t], W=[t_.t], eng=eng)
                        tt(u_.h[:, 0, :], x.h[:, 1, :], AA.h[:, d, 1, 0, :], ALU.mult, R=[x.t, AA.t], W=[u_.t], eng=eng)
                        tt(u_.h[:, 1, :], x.h[:, 0, :], AA.h[:, d, 1, 1, :], ALU.mult, R=[x.t, AA.t], W=[u_.t], eng=eng)
                        tt(t_.h[:], t_.h[:], u_.h[:], ALU.add, R=[t_.t, u_.t], W=[t_.t], eng=eng)
                        tt(x.h[:], t_.h[:], Ssb.h[:, :, d * 32:(d + 1) * 32, c], ALU.add, R=[t_.t, SsT[d]], W=[x.t], eng=eng)
                        cp(Ssb.h[:, :, d * 32:(d + 1) * 32, c], x.h[:], R=[x.t], W=[SsT[d]], eng=eng)
                bdl = Ring([sb("bdl%d" % i, [128, TC, 2, 128], BF16) for i in range(2)])
                cll = Ring([sb("cll%d" % i, [64, TC, 2, 4, 2, 16], BF16) for i in range(2)])
                dsk = sb("dsk", [128, L, 8])
                ld(dsk.h[:], IN["s5_dT"], W=[dsk.t])
                yfr = Ring([sb("yf%d" % i, [128, 512]) for i in range(2)])
                zbr = Ring([sb("zb%d" % i, [128, 512], BF16) for i in range(2)])
                for tl in range(8):
                    ut = utr.next(); bd = bdl.next(); cl = cll.next()
                    ld(ut.h[:], UT[tl * 128:(tl + 1) * 128, :], W=[ut.t])
                    ld(bd.h[:].rearrange("p k d m -> p k (d m)"), BDD[:, :, tl].rearrange("p k d m -> p k (d m)"), W=[bd.t])
                    for d_ in range(2):
                        ld(cl.h[:, :, d_].rearrange("n k g r o -> n k (g r o)"), CLD[:, :, d_, tl].rearrange("n k g r o -> n k (g r o)"), W=[cl.t])
                    for (t0, n) in TILES:
                        if t0 == 0 and not need_ctx:
                            continue
                        c0 = t0 // TC
                        nch = n // TC
                        y = PSR.next()
                        yv = y.h[:, 0:n].rearrange("p (c j) -> p c j", j=TC)
                        uv = ut.h[:, t0:t0 + n].rearrange("p (c j) -> p c j", j=TC)
                        ops = []
                        for tau in range(TC):
                            ops.append((yv[:, :, tau:TC], bd.h[:, tau, 0, :], uv[:, :, 0:TC - tau], tau == 0, {}))
                        for tau in range(TC):
                            ops.append((yv[:, :, 0:TC - tau], bd.h[:, tau, 1, :], uv[:, :, tau:TC], False, {}))
                        for gl in range(4):
                            g = tl * 4 + gl
                            yg = y.h[32 * gl:32 * gl + 16, 0:n].rearrange("p (c j) -> p c j", j=TC)
                            kw = {"tile_position": (0, 32 * gl)}
                            for j in range(TC):
                                for r in range(2):
                                    if c0 == 0:
                                        ops.append((yg[:, 1:nch, j], cl.h[:, j, 0, gl, r, :], Ssb.h[:, r, g, 0:nch - 1], False, kw))
                                    else:
                                        ops.append((yg[:, :, j], cl.h[:, j, 0, gl, r, :], Ssb.h[:, r, g, c0 - 1:c0 - 1 + nch], False, kw))
                                    lw = cl.h[:, TC - j - 1, 1, gl, r, :]
                                    if c0 == 0:
                                        ops.append((yg[:, 0:nch - 1, j], lw, Ssb.h[:, r, 32 + g, 1:nch], False, kw))
                                    elif c0 + nch == NCH:
                                        ops.append((yg[:, 0:nch - 1, j], lw, Ssb.h[:, r, 32 + g, c0 + 1:c0 + nch], False, kw))
                                        ops.append((yg[:, nch - 1:nch, j], lw, Ssb.h[:, r, 32 + g, 0:1], False, kw))
                                    else:
                                        ops.append((yg[:, :, j], lw, Ssb.h[:, r, 32 + g, c0 + 1:c0 + 1 + nch], False, kw))
                        for oi, (oa, la, ra, stt_, kw) in enumerate(ops):
                            last = oi == len(ops) - 1
                            mm(oa, la, ra, stt_, last, R=[bd.t, ut.t, cl.t, SsT[0], SsT[1]], W=[y.t], inc=last,
                               skip_group_check=True, **kw)
                        yf = yfr.next()
                        stt(yf.h[:, 0:n], ut.h[:, t0:t0 + n], dsk.h[:, li, tl:tl + 1], y.h[:, 0:n], ALU.mult, ALU.add,
                            R=[ut.t, dsk.t, y.t], W=[yf.t])
                        zb = zbr.next()
                        act(zb.h[:, 0:n], yf.h[:, 0:n], AF.Gelu_apprx_tanh, R=[yf.t], W=[zb.t])
                        st_(ZT[tl * 128:(tl + 1) * 128, t0:t0 + n], zb.h[:, 0:n], R=[zb.t])
                S.flush()

        def phase_merge(li):
            need_ctx = li < L - 1
            with contextlib.ExitStack() as ps_:
                def sb(name, shape, dt=F32):
                    return Buf(ps_.enter_context(nc.sbuf_tensor(_uniq(name), list(shape), dt)))

                def wl(name, key, kch, cols):
                    w = sb(name, [128, kch, cols], BF16)
                    src = IN[key][li].rearrange("(k p) c -> p k c", p=128)
                    for c0 in range(0, cols, 1024):
                        c1 = min(cols, c0 + 1024)
                        ldw(w.h[:, :, c0:c1], src[:, :, c0:c1], W=[w.t])
                    return w
                w_mo = wl("w_mo", "mla_w_o", 4, 1024)
                w_do = wl("w_do", "diff_w_o", 4, 1024)
                w_gl = wl("w_gl", "w_glu", 8, 2048)
                w_g = wl("w_g", "w_g", 8, 3072)
                w_o = wl("w_o", "w_out", 8, 1024)
                inr = Ring([sb("min%d" % i, [128, 24, 512], BF16) for i in range(1)])
                mtr = Ring([sb("mt%d" % i, [128, 8, 512], BF16) for i in range(1)])
                xsr = Ring([sb("mxs%d" % i, [128, 8, 512]) for i in range(1)])
                sgr = Ring([sb("sg%d" % i, [128, 512]) for i in range(5)])
                f32r = Ring([sb("mf%d" % i, [128, 512]) for i in range(4)])
                for (t0, n) in TILES:
                    if t0 == 0 and not need_ctx:
                        continue
                    which = 1 if t0 == 0 else 0
                    xin = inr.next()
                    ld(xin.h[:, 0:4, 0:n], MO.rearrange("(k p) t -> p k t", p=128)[:, :, t0:t0 + n], W=[xin.t])
                    ld(xin.h[:, 4:8, 0:n], DO.rearrange("(k p) t -> p k t", p=128)[:, :, t0:t0 + n], W=[xin.t])
                    ld(xin.h[:, 8:16, 0:n], ZT.rearrange("(k p) t -> p k t", p=128)[:, :, t0:t0 + n], W=[xin.t])
                    ld(xin.h[:, 16:24, 0:n], ATD.rearrange("(k p) t -> p k t", p=128)[:, :, t0:t0 + n], W=[xin.t])
                    xs = xsr.next()
                    ld(xs.h[:, :, 0:n], RSv[0][:, :, t0:t0 + n], W=[xs.t])
                    mt = mtr.next()
                    for kf in range(8):
                        def grp(w, kbase, nk, c0):
                            pb = PSR.next()
                            for k in range(nk):
                                mm(pb.h[:, 0:n], w.h[:, k, c0:c0 + 128], xin.h[:, kbase + k, 0:n], k == 0, k == nk - 1,
                                   R=[w.t, xin.t], W=[pb.t], inc=(k == nk - 1))
                            return pb
                        pg = [grp(w_g, 16, 8, j * 1024 + kf * 128) for j in range(3)]
                        sg = []
                        for j in range(3):
                            s_ = sgr.next()
                            act(s_.h[:, 0:n], pg[j].h[:, 0:n], AF.Sigmoid, R=[pg[j].t], W=[s_.t])
                            sg.append(s_)
                        pzb = grp(w_gl, 8, 8, 1024 + kf * 128)
                        szb = sgr.next()
                        act(szb.h[:, 0:n], pzb.h[:, 0:n], AF.Sigmoid, R=[pzb.t], W=[szb.t])
                        pza = grp(w_gl, 8, 8, kf * 128)
                        pym = grp(w_mo, 0, 4, kf * 128)
                        pyd = grp(w_do, 4, 4, kf * 128)
                        a = f32r.next(); b = f32r.next(); c = f32r.next()
                        tt(a.h[:, 0:n], pza.h[:, 0:n], szb.h[:, 0:n], ALU.mult, R=[pza.t, szb.t], W=[a.t])
                        tt(a.h[:, 0:n], a.h[:, 0:n], sg[2].h[:, 0:n], ALU.mult, R=[a.t, sg[2].t], W=[a.t], eng="pool")
                        tt(b.h[:, 0:n], pym.h[:, 0:n], sg[0].h[:, 0:n], ALU.mult, R=[pym.t, sg[0].t], W=[b.t])
                        tt(c.h[:, 0:n], pyd.h[:, 0:n], sg[1].h[:, 0:n], ALU.mult, R=[pyd.t, sg[1].t], W=[c.t])
                        tt(b.h[:, 0:n], b.h[:, 0:n], c.h[:, 0:n], ALU.add, R=[b.t, c.t], W=[b.t], eng="pool")
                        tt(mt.h[:, kf, 0:n], a.h[:, 0:n], b.h[:, 0:n], ALU.add, R=[a.t, b.t], W=[mt.t], eng="pool")
                    for kf in range(8):
                        pb = PSR.next()
                        for k in range(8):
                            mm(pb.h[:, 0:n], w_o.h[:, k, kf * 128:(kf + 1) * 128], mt.h[:, k, 0:n], k == 0, k == 7,
                               R=[w_o.t, mt.t], W=[pb.t], inc=(k == 7))
                        stt(xs.h[:, kf, 0:n], pb.h[:, 0:n], mvec(li, which, 16 + kf), xs.h[:, kf, 0:n], ALU.mult, ALU.add,
                            R=[pb.t, xs.t, modv.t], W=[xs.t])
                    st_(RSv[1][:, :, t0:t0 + n], xs.h[:, :, 0:n], R=[xs.t])
                S.flush()

        def phase_ffn(li, final):
            need_ctx = li < L - 1
            with contextlib.ExitStack() as ps_:
                def sb(name, shape, dt=F32):
                    return Buf(ps_.enter_context(nc.sbuf_tensor(_uniq(name), list(shape), dt)))
                w_up = sb("w_up", [128, 8, 2 * FH], BF16)
                w_dn = sb("w_dn", [128, NFT, D], BF16)
                src = IN["w_up"][li].rearrange("(k p) c -> p k c", p=128)
                for c0 in range(0, 2 * FH, 1024):
                    c1 = min(2 * FH, c0 + 1024)
                    ldw(w_up.h[:, :, c0:c1], src[:, :, c0:c1], W=[w_up.t])
                ldw(w_dn.h[:], IN["w_down"][li].rearrange("(k p) c -> p k c", p=128), W=[w_dn.t])
                cw = sb("cw", [128, L, 3, NFT]); cb = sb("cb", [128, L, NFT]); fn = sb("fn", [128, 8])
                ld(cw.h[:], IN["conv_wT"], W=[cw.t]); ld(cb.h[:], IN["conv_bT"], W=[cb.t]); ld(fn.h[:], IN["fnormT"], W=[fn.t])
                NT = 258
                xsr = Ring([sb("fxs%d" % i, [128, 8, NT]) for i in range(2)])
                sqr = Ring([sb("fsq%d" % i, [128, 8, NT], BF16) for i in range(1)])
                a2r = Ring([sb("fa2%d" % i, [128, 8, NT], BF16) for i in range(2)])
                tmpr = Ring([sb("ftm%d" % i, [128, NT]) for i in range(6)])
                rsr = Ring([sb("frs%d" % i, [128, NT]) for i in range(2)])
                htr = Ring([sb("fht%d" % i, [128, NFT, 256], BF16) for i in range(1)])
                otr = Ring([sb("fot%d" % i, [128, 8, 128]) for i in range(2)])
                ftiles = [(0, NCTX, 0, NCTX)] + [(NCTX + 256 * i, 256, NCTX, TT) for i in range(16)]
                for (t0, n, lo, hi) in ftiles:
                    if t0 == 0 and not need_ctx:
                        continue
                    which = 1 if t0 == 0 else 0
                    xs = xsr.next()
                    a0 = max(t0 - 1, lo); a1 = min(t0 + n + 1, hi)
                    off = a0 - (t0 - 1)
                    if off > 0:
                        mset(xs.h[:, :, 0:1], 1.0, W=[xs.t])
                    if a1 < t0 + n + 1:
                        mset(xs.h[:, :, NT - 1:NT], 1.0, W=[xs.t])
                    ld(xs.h[:, :, off:off + (a1 - a0)], RSv[1][:, :, a0:a1], W=[xs.t])
                    a2 = a2r.next()
                    modulate(rsr, xs, NT, li, which, 24, 32, lambda k: (a2.h[:, k, :], a2.t), tmpr, sqr)
                    if off > 0:
                        mset(a2.h[:, :, 0:1], 0.0, W=[a2.t])
                    if a1 < t0 + n + 1:
                        mset(a2.h[:, :, NT - 1:NT], 0.0, W=[a2.t])
                    ht = htr.next()
                    for m in range(NFT):
                        pa = PSR.next(); pg = PSR.next()
                        for k in range(8):
                            mm(pa.h[:, 0:NT], w_up.h[:, k, m * 128:(m + 1) * 128], a2.h[:, k, :], k == 0, k == 7,
                               R=[w_up.t, a2.t], W=[pa.t], inc=(k == 7))
                        for k in range(8):
                            mm(pg.h[:, 0:256], w_up.h[:, k, FH + m * 128:FH + (m + 1) * 128], a2.h[:, k, 1:257], k == 0, k == 7,
                               R=[w_up.t, a2.t], W=[pg.t], inc=(k == 7))
                        t1 = tmpr.next(); t2 = tmpr.next(); t3 = tmpr.next(); t4 = tmpr.next()
                        ts(t1.h[:, 0:256], pa.h[:, 1:257], cw.h[:, li, 1, m:m + 1], cb.h[:, li, m:m + 1], ALU.mult, ALU.add,
                           R=[pa.t, cw.t, cb.t], W=[t1.t])
                        stt(t2.h[:, 0:256], pa.h[:, 0:256], cw.h[:, li, 0, m:m + 1], t1.h[:, 0:256], ALU.mult, ALU.add,
                            R=[pa.t, cw.t, t1.t], W=[t2.t])
                        stt(t3.h[:, 0:256], pa.h[:, 2:258], cw.h[:, li, 2, m:m + 1], t2.h[:, 0:256], ALU.mult, ALU.add,
                            R=[pa.t, cw.t, t2.t], W=[t3.t])
                        act(t4.h[:, 0:256], t3.h[:, 0:256], AF.Gelu_apprx_tanh, R=[t3.t], W=[t4.t])
                        tt(ht.h[:, m, :], t4.h[:, 0:256], pg.h[:, 0:256], ALU.mult, R=[t4.t, pg.t], W=[ht.t])
                    for kf in range(8):
                        pb = PSR.next()
                        for k in range(NFT):
                            mm(pb.h[:, 0:256], w_dn.h[:, k, kf * 128:(kf + 1) * 128], ht.h[:, k, :], k == 0, k == NFT - 1,
                               R=[w_dn.t, ht.t], W=[pb.t], inc=(k == NFT - 1))
                        stt(xs.h[:, kf, 1:257], pb.h[:, 0:256], mvec(li, which, 40 + kf), xs.h[:, kf, 1:257], ALU.mult, ALU.add,
                            R=[pb.t, xs.t, modv.t], W=[xs.t])
                    if not final:
                        st_(RSv[0][:, :, t0:t0 + n], xs.h[:, :, 1:257], R=[xs.t])
                    else:
                        sq = sqr.next()
                        act(sq.h[:, :, 0:256], xs.h[:, :, 1:257], AF.Square, R=[xs.t], W=[sq.t])
                        pb = PSR.next()
                        for k in range(8):
                            mm(pb.h[:, 0:256], onesb.h[:], sq.h[:, k, 0:256], k == 0, k == 7, R=[onesb.t, sq.t], W=[pb.t], inc=(k == 7))
                        rt = tmpr.next(); rs = tmpr.next()
                        act(rt.h[:, 0:256], pb.h[:, 0:256], AF.Sqrt, R=[pb.t, epsb.t], W=[rt.t], bias=epsb.h[:, 0:1], scale=1.0 / D)
                        rcp(rs.h[:, 0:256], rt.h[:, 0:256], R=[rt.t], W=[rs.t])
                        for k in range(8):
                            stt(xs.h[:, k, 1:257], xs.h[:, k, 1:257], fn.h[:, k:k + 1], rs.h[:, 0:256], ALU.mult, ALU.mult,
                                R=[xs.t, rs.t, fn.t], W=[xs.t])
                        for j in range(2):
                            ot = otr.next()
                            for kk in range(2):
                                pb = PSR.next()
                                for k4 in range(4):
                                    k = kk * 4 + k4
                                    tr(pb.h[:, k4 * 128:(k4 + 1) * 128], xs.h[:, k, 1 + j * 128:1 + (j + 1) * 128], identf.h[:],
                                       R=[xs.t, identf.t], W=[pb.t], inc=(k4 == 3))
                                if kk == 0:
                                    act(ot.h[:, 0:4, :], pb.h[:, :].rearrange("p (a b) -> p a b", a=4), AF.Identity, R=[pb.t], W=[ot.t])
                                else:
                                    cp(ot.h[:, 4:8, :], pb.h[:, :].rearrange("p (a b) -> p a b", a=4), R=[pb.t], W=[ot.t])
                            r0 = t0 - NCTX + j * 128
                            st_(out_d[r0:r0 + 128, :], ot.h[:].rearrange("p a b -> p (a b)"), R=[ot.t])
                S.flush()

        order = ["proj", "mla", "diff", "s5", "merge", "ffn"]
        stop_at = None if upto == "all" else upto
        done = False
        for li in layers:
            for ph in order:
                if ph == "proj":
                    phase_proj(li)
                elif ph == "mla":
                    phase_mla(li)
                elif ph == "diff":
                    phase_diff(li)
                elif ph == "s5":
                    phase_s5(li)
                elif ph == "merge":
                    phase_merge(li)
                elif ph == "ffn":
                    phase_ffn(li, final=(li == L - 1))
                if stop_at == (li, ph):
                    done = True
                    break
            if done:
                break
        S.flush()
    return nc


_NC_CACHE = {}


def kernel(**inputs):
    sh = _shared_inputs(inputs)
    cores = [_core_inputs(inputs, b) for b in range(8)]
    shapes = {k: v.shape for k, v in sh.items()}
    shapes.update({k: v.shape for k, v in cores[0].items()})
    key = "main"
    if key not in _NC_CACHE:
        _NC_CACHE[key] = build(shapes)
    nc = _NC_CACHE[key]
    in_maps = [dict(sh, **cores[b]) for b in range(8)]
    res = run_bass_kernel_spmd(nc, in_maps, core_ids=list(range(8)))
    out = np.stack([np.asarray(res.results[b]["out"], dtype=np.float32) for b in range(8)], 0)
    return out
```
